# Optimizing a Trainium2 kernel written in Bass

```python
import math
import jax, jax.numpy as jnp
from jax import lax
import numpy as np

D_MODEL = 1024
BATCH = 8
SEQ = 2048
DEPTH = 2

GRID_W = 64
CTX_LEN = 256
EPS = 1e-6

GDN_HEADS = 8
GDN_DK = 64
GDN_DV = 64
GDN_KEY_W = GDN_HEADS * GDN_DK
GDN_VAL_W = GDN_HEADS * GDN_DV
CONV_W = 2 * GDN_KEY_W + GDN_VAL_W
CONV_K = 5
CHUNK = 64

MLA_HEADS = 8
QK_NOPE = 64
QK_ROPE = 32
V_DIM = 64
Q_LORA = 256
KV_LORA = 128
ROPE_THETA = 10000.0
AXIS_DIM = QK_ROPE // 2
Q_BLOCK = 128

MIX_WIDTH = GDN_VAL_W + MLA_HEADS * V_DIM
FFN_HIDDEN = -(-8 * D_MODEL // (3 * 256)) * 256

IN_SPLITS = (CONV_W, GDN_VAL_W, 2 * GDN_HEADS, 2 * GDN_HEADS, Q_LORA, KV_LORA, QK_ROPE)
IN_COLS = sum(IN_SPLITS)

kernel_name = 'hybrid_gdn_mla_dit_block'


def rmsnorm(x, g):
    xf = x.astype(jnp.float32)
    y = xf * lax.rsqrt(jnp.mean(xf * xf, axis=-1, keepdims=True) + EPS)
    return (y * g.astype(jnp.float32)).astype(x.dtype)


def l2norm(x):
    xf = x.astype(jnp.float32)
    return xf * lax.rsqrt(jnp.sum(xf * xf, axis=-1, keepdims=True) + EPS)


def modulate(x, shift, scale):
    return x * (1 + scale) + shift


def swiglu(h, w_gate, w_up, w_down):
    return (jax.nn.silu(h @ w_gate) * (h @ w_up)) @ w_down


def split_cols(p):
    out, start = [], 0
    for size in IN_SPLITS:
        out.append(p[..., start:start + size])
        start += size
    return out


def axial_rope_tables(T, dtype):
    rows = T // GRID_W
    row = jnp.repeat(jnp.arange(rows), GRID_W).astype(jnp.float32)
    col = jnp.tile(jnp.arange(GRID_W), rows).astype(jnp.float32)
    inv_freq = ROPE_THETA ** (-jnp.arange(0, AXIS_DIM, 2, dtype=jnp.float32) / AXIS_DIM)

    def axis_angles(pos):
        ang = pos[:, None] * inv_freq[None, :]
        return jnp.concatenate([ang, ang], axis=-1)

    ang = jnp.concatenate([axis_angles(row), axis_angles(col)], axis=-1)
    return jnp.cos(ang).astype(dtype), jnp.sin(ang).astype(dtype)


def rotate_half_axial(x):
    quarter = AXIS_DIM // 2

    def rh(t):
        return jnp.concatenate([-t[..., quarter:], t[..., :quarter]], axis=-1)

    return jnp.concatenate([rh(x[..., :AXIS_DIM]), rh(x[..., AXIS_DIM:])], axis=-1)


def apply_rope(x, cos, sin):
    return x * cos + rotate_half_axial(x) * sin


def gdn_qkv(qkv, conv_w):
    B, T, _ = qkv.shape
    u = lax.conv_general_dilated(qkv, conv_w[:, None, :].astype(qkv.dtype), window_strides=(1,),
                                 padding=((CONV_K // 2, CONV_K // 2),),
                                 dimension_numbers=('NWC', 'WIO', 'NWC'), feature_group_count=CONV_W)
    u = jax.nn.silu(u)
    q = l2norm(u[..., :GDN_KEY_W].reshape(B, T, GDN_HEADS, GDN_DK)) * (GDN_DK ** -0.5)
    k = l2norm(u[..., GDN_KEY_W:2 * GDN_KEY_W].reshape(B, T, GDN_HEADS, GDN_DK))
    v = u[..., 2 * GDN_KEY_W:].reshape(B, T, GDN_HEADS, GDN_DV)
    return q, k, v


def gdn_gates(a, b, a_log, dt_bias, d):
    sl = slice(d * GDN_HEADS, (d + 1) * GDN_HEADS)
    g = -jnp.exp(a_log[d].astype(jnp.float32)) * jax.nn.softplus(
        a[..., sl].astype(jnp.float32) + dt_bias[d].astype(jnp.float32))
    beta = jax.nn.sigmoid(b[..., sl].astype(jnp.float32))
    return g, beta


def gdn_chunked(q, k, v, g, beta, s0):
    B, T, H, DK = q.shape
    DV = v.shape[-1]
    n = T // CHUNK

    def chunks(t):
        t = jnp.moveaxis(t.astype(jnp.float32), 2, 1)
        return t.reshape(B, H, n, CHUNK, *t.shape[3:])

    q, k, v, g, beta = chunks(q), chunks(k), chunks(v), chunks(g), chunks(beta)
    gc = jnp.cumsum(g, axis=-1)
    idx = jnp.arange(CHUNK)
    incl = idx[:, None] >= idx[None, :]
    strict = idx[:, None] > idx[None, :]
    decay = jnp.exp(jnp.where(incl, gc[..., :, None] - gc[..., None, :], -jnp.inf))
    kb = k * beta[..., None]
    a_mat = jnp.where(strict, jnp.einsum('bhnid,bhnjd->bhnij', kb, k) * decay, 0.0)
    eye = jnp.eye(CHUNK, dtype=jnp.float32)
    rhs = jnp.concatenate([v * beta[..., None], kb * jnp.exp(gc)[..., None]], axis=-1)
    sol = lax.linalg.triangular_solve(a_mat + eye, rhs, left_side=True, lower=True)
    u, w = sol[..., :DV], sol[..., DV:]
    qk = jnp.einsum('bhnid,bhnjd->bhnij', q, k) * decay
    qg = q * jnp.exp(gc)[..., None]
    kg = k * jnp.exp(gc[..., -1:] - gc)[..., None]
    g_last = jnp.exp(gc[..., -1])

    def step(S, xs):
        qk_i, qg_i, kg_i, u_i, w_i, gl_i = xs
        v_new = u_i - jnp.einsum('bhck,bhkv->bhcv', w_i, S)
        o_i = jnp.einsum('bhck,bhkv->bhcv', qg_i, S) + jnp.einsum('bhij,bhjv->bhiv', qk_i, v_new)
        S = S * gl_i[..., None, None] + jnp.einsum('bhck,bhcv->bhkv', kg_i, v_new)
        return S, o_i

    xs = tuple(jnp.moveaxis(t, 2, 0) for t in (qk, qg, kg, u, w, g_last))
    s_final, o = lax.scan(step, s0.astype(jnp.float32), xs)
    o = jnp.moveaxis(o, 0, 2).reshape(B, H, T, DV)
    return jnp.moveaxis(o, 1, 2), s_final


def gdn_bidir(q, k, v, a, b, a_log, dt_bias, s0_f, s0_b):
    g_f, beta_f = gdn_gates(a, b, a_log, dt_bias, 0)
    g_b, beta_b = gdn_gates(a, b, a_log, dt_bias, 1)
    o_f, s_f = gdn_chunked(q, k, v, g_f, beta_f, s0_f)
    flip = lambda t: jnp.flip(t, axis=1)
    o_b, s_b = gdn_chunked(flip(q), flip(k), flip(v), flip(g_b), flip(beta_b), s0_b)
    return o_f + flip(o_b), s_f, s_b


def gdn_output(o, z, gdn_norm_g):
    B, T = z.shape[:2]
    y = rmsnorm(o, gdn_norm_g).astype(z.dtype) * jax.nn.silu(z.reshape(B, T, GDN_HEADS, GDN_DV))
    return y.reshape(B, T, GDN_VAL_W)


def mla_queries(c_q, q_norm_g, w_uq, cos, sin):
    B, T, _ = c_q.shape
    q = (rmsnorm(c_q, q_norm_g) @ w_uq).reshape(B, T, MLA_HEADS, QK_NOPE + QK_ROPE)
    q_nope, q_pe = q[..., :QK_NOPE], q[..., QK_NOPE:]
    if cos is not None:
        q_pe = apply_rope(q_pe, cos[:, None, :], sin[:, None, :])
    return q_nope, q_pe


def mla_keys(c_kv, k_rope, kv_norm_g, w_ukv, cos, sin):
    B, T, _ = c_kv.shape
    kv = (rmsnorm(c_kv, kv_norm_g) @ w_ukv).reshape(B, T, MLA_HEADS, QK_NOPE + V_DIM)
    k_nope, v = kv[..., :QK_NOPE], kv[..., QK_NOPE:]
    k_pe = k_rope
    if cos is not None:
        k_pe = apply_rope(k_pe, cos, sin)
    return k_nope, k_pe, v


def attend(q_nope, q_pe, k_nope, k_pe, v):
    s = (jnp.einsum('bqhd,bkhd->bhqk', q_nope, k_nope)
         + jnp.einsum('bqhd,bkd->bhqk', q_pe, k_pe)) * ((QK_NOPE + QK_ROPE) ** -0.5)
    p = jax.nn.softmax(s.astype(jnp.float32), axis=-1).astype(v.dtype)
    return jnp.einsum('bhqk,bkhd->bqhd', p, v)


def latent_attention(q_nope, q_pe, k_nope, k_pe, v):
    B, T = q_nope.shape[:2]
    nblk = T // Q_BLOCK

    def blocks(t):
        return jnp.moveaxis(t.reshape(B, nblk, Q_BLOCK, *t.shape[2:]), 1, 0)

    o = lax.map(lambda qb: attend(qb[0], qb[1], k_nope, k_pe, v), (blocks(q_nope), blocks(q_pe)))
    return jnp.moveaxis(o, 0, 1).reshape(B, T, MLA_HEADS * V_DIM)


def hybrid_mixer(h, hc, w_in, conv_w, a_log, dt_bias, gdn_norm_g, q_norm_g, w_uq, kv_norm_g, w_ukv,
                 w_out, cos, sin, need_ctx_out):
    B, S, _ = h.shape
    qkv, z, a, b, c_q, c_kv, k_rope = split_cols(h @ w_in)
    qkv_c, z_c, a_c, b_c, c_q_c, c_kv_c, k_rope_c = split_cols(hc @ w_in)

    q, k, v = gdn_qkv(qkv, conv_w)
    qc, kc, vc = gdn_qkv(qkv_c, conv_w)
    s0 = jnp.zeros((B, GDN_HEADS, GDN_DK, GDN_DV), jnp.float32)
    o_c, s_f, s_b = gdn_bidir(qc, kc, vc, a_c, b_c, a_log, dt_bias, s0, s0)
    o, _, _ = gdn_bidir(q, k, v, a, b, a_log, dt_bias, s_f, s_b)
    y_gdn = gdn_output(o, z, gdn_norm_g)

    qn, qp = mla_queries(c_q, q_norm_g, w_uq, cos, sin)
    kn, kp, vm = mla_keys(c_kv, k_rope, kv_norm_g, w_ukv, cos, sin)
    knc, kpc, vmc = mla_keys(c_kv_c, k_rope_c, kv_norm_g, w_ukv, None, None)
    kn_all = jnp.concatenate([knc, kn], axis=1)
    kp_all = jnp.concatenate([kpc, kp], axis=1)
    v_all = jnp.concatenate([vmc, vm], axis=1)
    y_mla = latent_attention(qn, qp, kn_all, kp_all, v_all)

    y = jnp.concatenate([y_gdn, y_mla], axis=-1) @ w_out
    if need_ctx_out:
        qnc, qpc = mla_queries(c_q_c, q_norm_g, w_uq, None, None)
        yc_mla = attend(qnc, qpc, knc, kpc, vmc).reshape(B, hc.shape[1], MLA_HEADS * V_DIM)
        yc = jnp.concatenate([gdn_output(o_c, z_c, gdn_norm_g), yc_mla], axis=-1) @ w_out
    else:
        yc = None
    return y, yc


def setup_inputs(seed: int = 0) -> dict:
    key = jax.random.key(seed)
    ks = jax.random.split(key, 24)
    f32 = jnp.float32

    def nrm(k, shape, fan_in, scale=1.0):
        return jax.random.normal(k, shape, f32) * (scale * fan_in ** -0.5)

    def gain(k, shape):
        return 1.0 + 0.02 * jax.random.normal(k, shape, f32)

    dt = jnp.exp(jax.random.uniform(ks[9], (DEPTH, 2, GDN_HEADS), f32, math.log(1e-3), math.log(1e-1)))
    return {
        'x': jax.random.normal(ks[0], (BATCH, SEQ, D_MODEL), f32),
        'c': jax.random.normal(ks[1], (BATCH, D_MODEL), f32),
        'ctx': jax.random.normal(ks[2], (BATCH, CTX_LEN, D_MODEL), f32),
        'c_ctx': jax.random.normal(ks[3], (D_MODEL,), f32),
        'w_mod': nrm(ks[4], (DEPTH, D_MODEL, 6 * D_MODEL), D_MODEL, 0.5),
        'b_mod': 0.02 * jax.random.normal(ks[5], (DEPTH, 6 * D_MODEL), f32),
        'norm1_g': gain(ks[6], (DEPTH, D_MODEL)),
        'norm2_g': gain(ks[7], (DEPTH, D_MODEL)),
        'w_in': nrm(ks[8], (DEPTH, D_MODEL, IN_COLS), D_MODEL),
        'conv_w': nrm(ks[10], (DEPTH, CONV_K, CONV_W), CONV_K),
        'a_log': jnp.log(jax.random.uniform(ks[11], (DEPTH, 2, GDN_HEADS), f32, 1.0, 16.0)),
        'dt_bias': dt + jnp.log(-jnp.expm1(-dt)),
        'gdn_norm_g': gain(ks[12], (DEPTH, GDN_DV)),
        'q_norm_g': gain(ks[13], (DEPTH, Q_LORA)),
        'w_uq': nrm(ks[14], (DEPTH, Q_LORA, MLA_HEADS * (QK_NOPE + QK_ROPE)), Q_LORA),
        'kv_norm_g': gain(ks[15], (DEPTH, KV_LORA)),
        'w_ukv': nrm(ks[16], (DEPTH, KV_LORA, MLA_HEADS * (QK_NOPE + V_DIM)), KV_LORA),
        'w_out': nrm(ks[17], (DEPTH, MIX_WIDTH, D_MODEL), MIX_WIDTH),
        'w_gate': nrm(ks[18], (DEPTH, D_MODEL, FFN_HIDDEN), D_MODEL),
        'w_up': nrm(ks[19], (DEPTH, D_MODEL, FFN_HIDDEN), D_MODEL),
        'w_down': nrm(ks[20], (DEPTH, FFN_HIDDEN, D_MODEL), FFN_HIDDEN),
        'final_norm_g': gain(ks[21], (D_MODEL,)),
    }


def reference(x, c, ctx, c_ctx, w_mod, b_mod, norm1_g, norm2_g, w_in, conv_w, a_log, dt_bias, gdn_norm_g,
              q_norm_g, w_uq, kv_norm_g, w_ukv, w_out, w_gate, w_up, w_down, final_norm_g):
    S = x.shape[1]
    cos, sin = axial_rope_tables(S, x.dtype)
    silu_c = jax.nn.silu(c)
    silu_cc = jax.nn.silu(c_ctx)
    for l in range(DEPTH):
        last = l == DEPTH - 1
        mod = (silu_c @ w_mod[l] + b_mod[l])[:, None, :]
        mod_c = silu_cc @ w_mod[l] + b_mod[l]
        sh1, sc1, g1, sh2, sc2, g2 = jnp.split(mod, 6, axis=-1)
        sh1c, sc1c, g1c, sh2c, sc2c, g2c = jnp.split(mod_c, 6, axis=-1)
        h = modulate(rmsnorm(x, norm1_g[l]), sh1, sc1)
        hc = modulate(rmsnorm(ctx, norm1_g[l]), sh1c, sc1c)
        y, yc = hybrid_mixer(h, hc, w_in[l], conv_w[l], a_log[l], dt_bias[l], gdn_norm_g[l], q_norm_g[l],
                             w_uq[l], kv_norm_g[l], w_ukv[l], w_out[l], cos, sin, not last)
        x = x + g1 * y
        h2 = modulate(rmsnorm(x, norm2_g[l]), sh2, sc2)
        x = x + g2 * swiglu(h2, w_gate[l], w_up[l], w_down[l])
        if not last:
            ctx = ctx + g1c * yc
            hc2 = modulate(rmsnorm(ctx, norm2_g[l]), sh2c, sc2c)
            ctx = ctx + g2c * swiglu(hc2, w_gate[l], w_up[l], w_down[l])
    return rmsnorm(x, final_norm_g)
```

```python
import contextlib
import numpy as np
import concourse.bass as bass
import concourse.mybir as mybir
from concourse.bass_utils import run_bass_kernel_spmd

F32 = mybir.dt.float32
BF16 = mybir.dt.bfloat16
ALU = mybir.AluOpType
AF = mybir.ActivationFunctionType
AX = mybir.AxisListType

ENGS = ("pe", "act", "dve", "pool", "sp")

D = 1024
T = 2048
TC = 256
NT = 18
NTOK = 2304
NL = 2
FF = 2816
EPS = 1e-6
WIN_COLS = 2656
C_Z, C_AB, C_CQ, C_CKV, C_KR, C_KRP = 1536, 2048, 2080, 2336, 2464, 2560


class Res:
    __slots__ = ("name", "lw", "rd")

    def __init__(self, name=""):
        self.name = name
        self.lw = None
        self.rd = []


class Op:
    __slots__ = ("eng", "emit", "deps", "dma", "slot", "semkey", "semval", "signal", "waits", "clock", "gidx")


class Sched:
    def __init__(self, nc, dma_slots=None):
        self.nc = nc
        self.ops = []
        self.by_eng = {e: [] for e in ENGS}
        self.dma_slots = dma_slots or {"sp": 8, "pool": 6, "act": 2}
        self.dma_cnt = {q: 0 for q in self.dma_slots}
        self.slot_last = {}
        self.last_op = {}
        self.pending_bar = {}

    def barrier(self):
        bar = list(self.last_op.values()) + list(self.slot_last.values())
        for e in ENGS:
            self.pending_bar[e] = bar

    def add(self, eng, emit, reads=(), writes=(), dma=False):
        op = Op()
        op.eng = eng
        op.emit = emit
        op.dma = dma
        op.signal = False
        op.gidx = len(self.ops)
        deps = {}
        for r in reads:
            if r.lw is not None:
                deps[r.lw] = "raw"
        for w in writes:
            if w.lw is not None:
                deps.setdefault(w.lw, "waw")
            last_rd = {}
            for o in w.rd:
                if o.dma:
                    deps.setdefault(o, "war")
                else:
                    last_rd[o.eng] = o
            for o in last_rd.values():
                deps.setdefault(o, "war")
        bar = self.pending_bar.pop(eng, None)
        if bar:
            for o in bar:
                deps[o] = "raw"
        if dma:
            n = self.dma_cnt[eng]
            self.dma_cnt[eng] = n + 1
            slot = (eng, n % self.dma_slots[eng])
            op.slot = slot
            prev = self.slot_last.get(slot)
            if prev is not None:
                deps[prev] = "slot"
            self.slot_last[slot] = op
            op.signal = True
        else:
            op.slot = None
        fd = []
        for d, kind in deps.items():
            if not d.dma and d.eng == eng:
                if eng == "pe" or kind == "war":
                    continue
            fd.append(d)
            d.signal = True
        op.deps = fd
        for r in reads:
            r.rd.append(op)
        for w in writes:
            w.lw = op
            w.rd = []
        self.ops.append(op)
        self.by_eng[eng].append(op)
        if not dma:
            self.last_op[eng] = op
        return op

    def finalize(self):
        cnt = {}
        for op in self.ops:
            if op.dma:
                key = ("dma",) + op.slot
                cnt[key] = cnt.get(key, 0) + 1
                op.semkey = key
                op.semval = 16 * cnt[key]
            elif op.signal:
                key = op.eng
                cnt[key] = cnt.get(key, 0) + 1
                op.semkey = key
                op.semval = cnt[key]
            else:
                op.semkey = None
                op.semval = 0
        known = {e: {} for e in ENGS}
        nw = 0
        for op in self.ops:
            kn = known[op.eng]
            waits = []
            for d in sorted(op.deps, key=lambda o: -o.gidx):
                if kn.get(d.semkey, 0) >= d.semval:
                    continue
                waits.append((d.semkey, d.semval))
                for k, v in d.clock.items():
                    if kn.get(k, 0) < v:
                        kn[k] = v
            op.waits = waits
            nw += len(waits)
            if op.semkey is not None:
                ck = dict(kn)
                ck[op.semkey] = op.semval
                op.clock = ck
            else:
                op.clock = None
        self.sem_keys = sorted(cnt.keys(), key=str)
        self.n_waits = nw
        self.sem_max = dict(cnt)
        return cnt

    def emit_all(self, final_wait_ops=()):
        nc = self.nc
        self.finalize()
        with contextlib.ExitStack() as st:
            sems = {}
            for k in self.sem_keys:
                nm = "s_" + "_".join(str(x) for x in (k if isinstance(k, tuple) else (k,)))
                sems[k] = st.enter_context(nc.semaphore(nm))
            block = st.enter_context(nc.Block())

            def run(eng_name):
                def body(eng):
                    for op in self.by_eng[eng_name]:
                        for (k, v) in op.waits:
                            eng.wait_ge(sems[k], v)
                        inst = op.emit(eng)
                        if op.semkey is not None:
                            inst.then_inc(sems[op.semkey], 16 if op.dma else 1)
                    if eng_name == "sp":
                        for op in final_wait_ops:
                            eng.wait_ge(sems[op.semkey], op.semval)
                return body

            block.tensor(run("pe"))
            block.scalar(run("act"))
            block.vector(run("dve"))
            block.gpsimd(run("pool"))
            block.sync(run("sp"))


class Rot:
    def __init__(self, items):
        self.items = items
        self.i = 0

    def next(self):
        it = self.items[self.i % len(self.items)]
        self.i += 1
        return it


class Builder:
    def __init__(self, mixer=True, layers=NL, dumps=(), ffn=True, stop=None):
        self.stop = stop
        self.nc = bass.Bass("TRN2", target_bir_lowering=False)
        self.S = Sched(self.nc)
        self.mixer = mixer
        self.ffn = ffn
        self.layers = layers
        self.dumps = set(dumps)
        self.out_ops = []
        self.dq = 0

    def op(self, eng, method, reads, writes, *args, **kw):
        return self.S.add(eng, lambda e: getattr(e, method)(*args, **kw), reads, writes)

    def dma(self, q, out, in_, reads, writes):
        return self.S.add(q, lambda e: e.dma_start(out=out, in_=in_), reads, writes, dma=True)

    def dram_in(self, name, shape, dt=F32):
        return self.nc.dram_tensor(name, list(shape), dt, kind="ExternalInput").ap()

    def dump(self, name, src_ap, shape, reads, dt=F32):
        if name not in self.dumps:
            return
        o = self.nc.dram_tensor("dbg_" + name, list(shape), dt, kind="ExternalOutput").ap()
        self.out_ops.append(self.dma("sp", o, src_ap, reads, []))

    def mm(self, out, lhsT, rhs, start, stop, reads, writes):
        return self.S.add("pe", lambda e: e.matmul(out, lhsT=lhsT, rhs=rhs, start=start, stop=stop), reads, writes)

    def tr(self, out, in_, ident, reads, writes):
        return self.S.add("pe", lambda e: e.transpose(out, in_, ident), reads, writes)

    def build(self):
        nc = self.nc
        with contextlib.ExitStack() as st:
            self.st = st
            self.declare_io()
            self.alloc_global()
            self.load_inputs()
            self.consts()
            for l in range(self.layers):
                self.phase_mod(l)
            for l in range(self.layers):
                last = l == NL - 1
                if self.mixer:
                    self.phase_mixer(l, last)
                    if self.stop:
                        break
                if self.ffn:
                    self.phase_ffn(l, last)
            if not self.stop:
                self.phase_final()
            self.S.emit_all(self.out_ops)
        return nc

    def sb(self, name, shape, dt, st=None):
        t = (st or self.st).enter_context(self.nc.sbuf_tensor(name, list(shape), dt))
        return t, Res(name)

    def ps(self, name, shape, dt=F32, st=None):
        t = (st or self.st).enter_context(self.nc.psum_tensor(name, list(shape), dt))
        return t, Res(name)

    def declare_io(self):
        d = self.dram_in
        self.x_d = d("x", [T, D])
        self.ctx_d = d("ctx", [TC, D])
        self.cT_d = d("cT", [128, 16])
        self.wmod_d = d("w_mod", [NL, D, 6 * D])
        self.bmodT_d = d("b_modT", [NL, 128, 48])
        self.n1T_d = d("n1T", [NL, 128, 8])
        self.n2T_d = d("n2T", [NL, 128, 8])
        self.fng_d = d("fng", [128, D])
        self.win_d = d("w_in_ext", [NL, D, WIN_COLS])
        self.convT_d = d("convT", [NL, 128, 60])
        self.alog_d = d("alog_bc", [NL, 128, 16])
        self.dtb_d = d("dtb_bc", [NL, 128, 16])
        self.gdng_d = d("gdng_bc", [NL, 128, 512])
        self.qngT_d = d("qngT", [NL, 128, 2])
        self.kvngT_d = d("kvngT", [NL, 128, 1])
        self.wuq_d = d("w_uq_ext", [NL, 256, 8 * 192])
        self.wukvk_d = d("w_ukv_k", [NL, 128, 512])
        self.wukvv_d = d("w_ukv_v", [NL, 128, 512])
        self.wout_d = d("w_out", [NL, D, D])
        self.wg_d = d("w_gate", [NL, D, FF])
        self.wu_d = d("w_up", [NL, D, FF])
        self.wd_d = d("w_down", [NL, FF, D])
        self.cosT_d = d("cosT", [128, T])
        self.sinT_d = d("sinT", [128, T])
        self.out_d = self.nc.dram_tensor("out", [T, D], F32, kind="ExternalOutput").ap()

    def alloc_global(self):
        self.xs, _ = self.sb("xs", [128, NT, D], F32)
        self.Rx = [Res(f"x{t}") for t in range(NT)]
        self.identf, self.Ridf = self.sb("identf", [128, 128], F32)
        self.identb, self.Ridb = self.sb("identb", [128, 128], BF16)
        self.onesf, self.Ronesf = self.sb("onesf", [128, 128], F32)
        self.onesb, self.Ronesb = self.sb("onesb", [128, 128], BF16)
        self.cT, self.RcT = self.sb("cT_sb", [128, 8, 2], F32)
        self.scT, self.RscT = self.sb("scT", [128, 8, 2], F32)
        self.modT = []
        self.vec = []
        for l in range(NL):
            m, r = self.sb(f"modT{l}", [128, 48, 2], F32)
            self.modT.append((m, r))
            v = {}
            for nm in ("gain1", "gain2"):
                v[nm] = self.sb(f"{nm}_{l}", [128, 8, 2], F32)
            v["bmodT"] = self.sb(f"bmodT{l}", [128, 48], F32)
            v["n1T"] = self.sb(f"n1T{l}", [128, 8], F32)
            v["n2T"] = self.sb(f"n2T{l}", [128, 8], F32)
            self.vec.append(v)
        self.gbc = {}
        for kind in (0, 1):
            self.gbc[kind] = self.sb(f"gbc{kind}", [128, D], F32)
        self.mk = {}
        for nm in ("LE", "GT", "GE", "LT", "SAME", "SL", "SU", "IL", "IU"):
            self.mk[nm] = self.sb("mk_" + nm, [128, 128], F32)
        self.CI, self.RCI = self.sb("mk_CI", [128, 2], F32)
        self.epsc, self.Reps = self.sb("epsc", [128, 1], F32)
        self.onec, self.Rone = self.sb("onec", [128, 1], F32)
        self.stat, _ = self.sb("stat", [128, 3, NT], F32)
        self.Rss = [Res(f"ss{t}") for t in range(NT)]
        self.Rln = Res("ln")
        self.Rrs = Res("rs")
        self.psf = Rot([self.ps(f"psf{i}", [128, 512], F32) for i in range(6)])
        self.psb = Rot([self.ps(f"psb{i}", [128, 1024], BF16) for i in range(2)])

    def load_inputs(self):
        xv = self.x_d.rearrange("(t p) d -> p t d", p=128)
        for t4 in range(0, 16, 4):
            self.dma("sp", self.xs[:, 2 + t4:2 + t4 + 4, :], xv[:, t4:t4 + 4, :], [], self.Rx[2 + t4:2 + t4 + 4])
        self.dma("sp", self.xs[:, 0:2, :], self.ctx_d.rearrange("(t p) d -> p t d", p=128), [], self.Rx[0:2])
        self.dma("sp", self.cT[:].rearrange("p k j -> p (k j)"), self.cT_d, [], [self.RcT])
        for l in range(NL):
            v = self.vec[l]
            self.dma("sp", v["bmodT"][0][:], self.bmodT_d[l], [], [v["bmodT"][1]])
            self.dma("sp", v["n1T"][0][:], self.n1T_d[l], [], [v["n1T"][1]])
            self.dma("sp", v["n2T"][0][:], self.n2T_d[l], [], [v["n2T"][1]])

    def consts(self):
        op = self.op
        op("pool", "memset", [], [self.Ridf], self.identf[:], 1.0)
        op("pool", "affine_select", [self.Ridf], [self.Ridf], out=self.identf[:], in_=self.identf[:],
           pattern=[[-1, 128]], compare_op=ALU.is_equal, fill=0.0, base=0, channel_multiplier=1)
        op("dve", "tensor_copy", [self.Ridf], [self.Ridb], out=self.identb[:], in_=self.identf[:])
        op("pool", "memset", [], [self.Ronesf], self.onesf[:], 1.0)
        op("pool", "memset", [], [self.Ronesb], self.onesb[:], 1.0)
        op("pool", "memset", [], [self.Reps], self.epsc[:], EPS)
        op("pool", "memset", [], [self.Rone], self.onec[:], 1.0)
        op("act", "activation", [self.RcT], [self.RscT], out=self.scT[:], in_=self.cT[:], func=AF.Silu)
        for nm, cmp, sg in (("LE", ALU.is_ge, -1), ("GT", ALU.is_gt, 1), ("GE", ALU.is_ge, 1), ("LT", ALU.is_gt, -1)):
            m, R = self.mk[nm]
            op("pool", "memset", [], [R], m[:], 1.0)
            op("pool", "affine_select", [R], [R], out=m[:], in_=m[:], pattern=[[-sg, 128]], compare_op=cmp, fill=0.0, base=0,
               channel_multiplier=sg)
        sm, Rsm = self.mk["SAME"]
        op("pool", "memset", [], [Rsm], sm[:], 0.0)
        op("pool", "memset", [Rsm], [Rsm], sm[0:64, 0:64], 1.0)
        op("pool", "memset", [Rsm], [Rsm], sm[64:128, 64:128], 1.0)
        for nm, src in (("SL", "GT"), ("SU", "LT"), ("IL", "GE"), ("IU", "LE")):
            m, R = self.mk[nm]
            op("pool", "tensor_tensor", [self.mk[src][1], Rsm], [R], out=m[:], in0=self.mk[src][0][:], in1=sm[:], op=ALU.mult)
        op("pool", "memset", [], [self.RCI], self.CI[:], 0.0)
        op("pool", "memset", [self.RCI], [self.RCI], self.CI[0:64, 0:1], 1.0)
        op("pool", "memset", [self.RCI], [self.RCI], self.CI[64:128, 1:2], 1.0)

    def phase_mod(self, l):
        op, mm = self.op, self.mm
        with contextlib.ExitStack() as st:
            wm = Rot([self.sb(f"wm{l}_{i}", [128, 8, 384], F32, st) for i in range(2)])
            pm, Rpm = self.psf.next()
            pmv = pm[:, 0:96].rearrange("p (c j) -> p c j", j=2)
            wv = self.wmod_d[l].rearrange("(k p) n -> p k n", p=128)
            for s in range(16):
                buf, Rb = wm.next()
                self.dma("sp", buf[:, 0:4, :], wv[:, 0:4, s * 384:(s + 1) * 384], [], [Rb])
                self.dma("sp", buf[:, 4:8, :], wv[:, 4:8, s * 384:(s + 1) * 384], [], [Rb])
                for j in range(3):
                    nch = s * 3 + j
                    for k in range(8):
                        mm(pmv[:, nch, :], buf[:, k, j * 128:(j + 1) * 128], self.scT[:, k, :], k == 0, k == 7,
                           [Rb, self.RscT], [Rpm])
            m, Rm = self.modT[l]
            v = self.vec[l]
            op("dve", "tensor_tensor", [Rpm, v["bmodT"][1]], [Rm], out=m[:], in0=pmv,
               in1=v["bmodT"][0][:].unsqueeze(2).to_broadcast([128, 48, 2]), op=ALU.add)
            for nm, base, nk in (("gain1", 8, "n1T"), ("gain2", 32, "n2T")):
                g, Rg = v[nm]
                op("dve", "scalar_tensor_tensor", [Rm, v[nk][1]], [Rg], out=g[:], in0=m[:, base:base + 8, :], scalar=1.0,
                   in1=v[nk][0][:].unsqueeze(2).to_broadcast([128, 8, 2]), op0=ALU.add, op1=ALU.mult)
            self.dump(f"modT{l}", m[:], [128, 48, 2], [Rm])
        self.S.barrier()

    def make_gbc(self, l, nm):
        op, mm = self.op, self.mm
        m, Rm = self.modT[l]
        base = 16 if nm == "g1" else 40
        with contextlib.ExitStack() as st:
            dg = Rot([self.sb(f"dg{l}_{nm}_{i}", [128, 128], F32, st) for i in range(3)])
            if True:
                for kind in (0, 1):
                    dst, Rd = self.gbc[kind]
                    for half in range(2):
                        p, Rp = self.psf.next()
                        for kk in range(4):
                            k = half * 4 + kk
                            d, Rdg = dg.next()
                            op("dve", "tensor_scalar", [self.Ridf, Rm], [Rdg], out=d[:], in0=self.identf[:],
                               scalar1=m[:, base + k, kind:kind + 1], scalar2=None, op0=ALU.mult)
                            mm(p[:, kk * 128:(kk + 1) * 128], self.onesf[:], d[:], True, True, [Rdg, self.Ronesf], [Rp])
                        op("act", "copy", [Rp], [Rd], out=dst[:, half * 512:(half + 1) * 512], in_=p[:])
        self.S.barrier()

    def norm_transpose(self, l, which, tiles, hT, RhT, xn_rot):
        op = self.op
        v = self.vec[l]
        gain, Rg = v["gain1" if which == 1 else "gain2"]
        m, Rm = self.modT[l]
        shbase = 0 if which == 1 else 24
        t0, t1 = tiles[0], tiles[-1] + 1
        ss = self.stat[:, 0, :]
        ln = self.stat[:, 1, :]
        rs = self.stat[:, 2, :]
        op("dve", "memset", [], self.Rss[t0:t1], ss[:, t0:t1], 0.0)
        for t in tiles:
            xn, Rxn = xn_rot.next()
            op("act", "activation", [self.Rx[t]], [Rxn, self.Rss[t]], out=xn[:], in_=self.xs[:, t, :], func=AF.Square,
               accum_out=ss[:, t:t + 1])
        op("act", "activation", self.Rss[t0:t1] + [self.Reps], [self.Rln], out=ln[:, t0:t1], in_=ss[:, t0:t1], func=AF.Ln,
           bias=self.epsc[:], scale=1.0 / D)
        op("act", "activation", [self.Rln], [self.Rrs], out=rs[:, t0:t1], in_=ln[:, t0:t1], func=AF.Exp, scale=-0.5)
        for t in tiles:
            kind = 1 if t < 2 else 0
            xn, Rxn = xn_rot.next()
            op("dve", "tensor_scalar", [self.Rx[t], self.Rrs], [Rxn], out=xn[:], in0=self.xs[:, t, :],
               scalar1=rs[:, t:t + 1], scalar2=None, op0=ALU.mult)
            pb, Rpb = self.psb.next()
            for k in range(8):
                self.tr(pb[:, k * 128:(k + 1) * 128], xn[:, k * 128:(k + 1) * 128], self.identb[:], [Rxn, self.Ridb], [Rpb])
            for k in range(8):
                dst = hT[:, k, t * 128:(t + 1) * 128]
                src = pb[:, k * 128:(k + 1) * 128]
                if k % 2 == 0:
                    op("dve", "tensor_scalar", [Rpb, Rg, Rm], [RhT[t]], out=dst, in0=src, scalar1=gain[:, k, kind:kind + 1],
                       scalar2=m[:, shbase + k, kind:kind + 1], op0=ALU.mult, op1=ALU.add)
                else:
                    op("act", "activation", [Rpb, Rg, Rm], [RhT[t]], out=dst, in_=src, func=AF.Identity,
                       bias=m[:, shbase + k, kind:kind + 1], scale=gain[:, k, kind:kind + 1])

    def phase_ffn(self, l, last):
        op, mm = self.op, self.mm
        self.make_gbc(l, "g2")
        tiles = list(range(2, NT)) if last else list(range(NT))
        groups = [(0, 8), (8, 7), (15, 7)]
        with contextlib.ExitStack() as st:
            hT, _ = self.sb(f"h2T{l}", [128, 8, NTOK], BF16, st)
            RhT = [Res(f"h2T{t}") for t in range(NT)]
            xn_rot = Rot([self.sb(f"xn2_{l}_{i}", [128, D], BF16, st) for i in range(2)])
            self.norm_transpose(l, 2, tiles, hT, RhT, xn_rot)
            self.dump(f"h2T{l}", hT[:], [128, 8, NTOK], RhT, BF16)
            self.dump(f"stat2_{l}", self.stat[:], [128, 3, NT], [self.Rrs, self.Rln] + self.Rss)
            self.dump(f"g2bc{l}", self.gbc[0][0][:], [128, D], [self.gbc[0][1]])
            wg, Rwg = self.sb(f"wg{l}", [128, 8, 1024], BF16, st)
            wu, Rwu = self.sb(f"wu{l}", [128, 8, 1024], BF16, st)
            wd, Rwd = self.sb(f"wd{l}", [128, 8, D], BF16, st)
            actT_rot = Rot([self.sb(f"actT{l}_{i}", [128, 8, 512], BF16, st) for i in range(2)])
            sg_rot = Rot([self.sb(f"sg{l}_{i}", [128, 512], BF16, st) for i in range(3)])
            tmp_rot = Rot([self.sb(f"ftmp{l}_{i}", [128, 512], F32, st) for i in range(3)])
            blocks = ([] if last else [(0, 2)]) + [(2 + 4 * i, 4) for i in range(4)]
            wgv = self.wg_d[l].rearrange("(k p) n -> p k n", p=128)
            wuv = self.wu_d[l].rearrange("(k p) n -> p k n", p=128)
            wdv = self.wd_d[l].rearrange("(k p) n -> p k n", p=128)
            for (g0, gn) in groups:
                c0 = g0 * 128
                cw = gn * 128
                for k in range(0, 8, 2):
                    self.dma("pool", wg[:, k:k + 2, 0:cw], wgv[:, k:k + 2, c0:c0 + cw], [], [Rwg])
                    self.dma("pool", wu[:, k:k + 2, 0:cw], wuv[:, k:k + 2, c0:c0 + cw], [], [Rwu])
                for k in range(0, gn, 4):
                    k2 = min(k + 4, gn)
                    self.dma("pool", wd[:, k:k2, :], wdv[:, g0 + k:g0 + k2, :], [], [Rwd])
                for (tb, ntl) in blocks:
                    ntok = ntl * 128
                    tok0 = tb * 128
                    actT, Ract = actT_rot.next()
                    for hc in range(gn):
                        pg, Rpg = self.psf.next()
                        pu, Rpu = self.psf.next()
                        for k in range(8):
                            mm(pg[:, 0:ntok], wg[:, k, hc * 128:(hc + 1) * 128], hT[:, k, tok0:tok0 + ntok], k == 0, k == 7,
                               [Rwg] + RhT[tb:tb + ntl], [Rpg])
                        for k in range(8):
                            mm(pu[:, 0:ntok], wu[:, k, hc * 128:(hc + 1) * 128], hT[:, k, tok0:tok0 + ntok], k == 0, k == 7,
                               [Rwu] + RhT[tb:tb + ntl], [Rpu])
                        sg, Rsg = sg_rot.next()
                        op("act", "activation", [Rpg], [Rsg], out=sg[:, 0:ntok], in_=pg[:, 0:ntok], func=AF.Silu)
                        op("dve", "tensor_tensor", [Rsg, Rpu], [Ract], out=actT[:, hc, 0:ntok], in0=pu[:, 0:ntok],
                           in1=sg[:, 0:ntok], op=ALU.mult)
                    for tt in range(ntl):
                        t = tb + tt
                        kind = 1 if t < 2 else 0
                        g2, Rg2 = self.gbc[kind]
                        for nb in range(2):
                            py, Rpy = self.psf.next()
                            for hc in range(gn):
                                mm(py[:], actT[:, hc, tt * 128:(tt + 1) * 128], wd[:, hc, nb * 512:(nb + 1) * 512], hc == 0,
                                   hc == gn - 1, [Ract, Rwd], [Rpy])
                            tmp, Rtmp = tmp_rot.next()
                            op("dve", "tensor_tensor", [Rpy, Rg2], [Rtmp], out=tmp[:], in0=py[:],
                               in1=g2[:, nb * 512:(nb + 1) * 512], op=ALU.mult)
                            op("dve", "tensor_tensor", [Rtmp, self.Rx[t]], [self.Rx[t]], out=self.xs[:, t, nb * 512:(nb + 1) * 512],
                               in0=self.xs[:, t, nb * 512:(nb + 1) * 512], in1=tmp[:], op=ALU.add)
            for t in tiles:
                self.dump(f"xffn{l}_{t}", self.xs[:, t, :], [128, D], [self.Rx[t]])
        self.S.barrier()

    def phase_mixer(self, l, last):
        nc = self.nc
        if not hasattr(self, "qkv_d"):
            self.qkv_d = nc.dram_tensor("scr_qkv", [3, NT, 128, 512], BF16).ap()
            self.zs_d = nc.dram_tensor("scr_zs", [NT, 128, 512], BF16).ap()
            self.of_d = nc.dram_tensor("scr_of", [NT, 128, 512], F32).ap()
            self.yg_d = nc.dram_tensor("scr_yg", [NT, 128, 512], BF16).ap()
            self.Rqkv = [[Res(f"qkv{g}_{t}") for t in range(NT)] for g in range(3)]
            self.Rzs = [Res(f"zs{t}") for t in range(NT)]
            self.Rof = [Res(f"of{t}") for t in range(NT)]
            self.Ryg = [Res(f"yg{t}") for t in range(NT)]
        with contextlib.ExitStack() as stM:
            M = {}
            M["cqnT"] = self.sb(f"cqnT{l}", [128, 2, NTOK], BF16, stM)
            M["ckvnT"] = self.sb(f"ckvnT{l}", [128, NTOK], BF16, stM)
            M["kpeT"] = self.sb(f"kpeT{l}", [128, NTOK], BF16, stM)
            M["gates"] = self.sb(f"gates{l}", [128, NT, 32], F32, stM)
            self.phase_B(l, last, M)
            if self.stop == "B":
                return
            self.phase_gdn(l, last, M)
            if self.stop == "G":
                return
            M["omla"] = self.sb(f"omla{l}", [128, NT, 512], BF16, stM)
            self.phase_mla(l, last, M)
            if self.stop == "A":
                return
            self.phase_out(l, last, M)
        self.S.barrier()

    def phase_B(self, l, last, M):
        op, mm = self.op, self.mm
        blocks = [(0, 256), (256, 512), (768, 512), (1280, 512), (1792, 512)]
        with contextlib.ExitStack() as st1:
            hT, _ = self.sb(f"hT{l}", [128, 8, NTOK], BF16, st1)
            RhT = [Res(f"hT{t}") for t in range(NT)]
            with contextlib.ExitStack() as stx:
                xn_rot = Rot([self.sb(f"xn1_{l}_{i}", [128, D], BF16, stx) for i in range(2)])
                self.norm_transpose(l, 1, list(range(NT)), hT, RhT, xn_rot)
            self.S.barrier()
            self.dump(f"hT{l}", hT[:], [128, 8, NTOK], RhT, BF16)
            convw, Rcw = self.sb(f"convw{l}", [128, 60], F32, st1)
            self.dma("sp", convw[:], self.convT_d[l], [], [Rcw])
            winv = self.win_d[l].rearrange("(k p) n -> p k n", p=128)

            def rh(tok0, ntok):
                return RhT[tok0 // 128:(tok0 + ntok) // 128]

            with contextlib.ExitStack() as st2:
                wq_rot = Rot([self.sb(f"wq{l}_{i}", [128, 8, 512], BF16, st2) for i in range(1)])
                cin_rot = Rot([self.sb(f"cin{l}_{i}", [128, 2312], F32, st2) for i in range(2)])
                for (cb, Rc) in cin_rot.items:
                    op("pool", "memset", [], [Rc], cb[:], 0.0)
                cacc, Rcacc = self.sb(f"cacc{l}", [128, 2312], F32, st2)
                siluT, RsT = self.sb(f"siluT{l}", [128, 4, NTOK], BF16, st2)
                tok_rot = Rot([self.sb(f"tokb{l}_{i}", [128, 512], BF16, st2) for i in range(3)])
                for g in range(3):
                    wq, Rwq = wq_rot.next()
                    for k0 in (0, 4):
                        self.dma("pool", wq[:, k0:k0 + 4, :], winv[:, k0:k0 + 4, g * 512:(g + 1) * 512], [], [Rwq])
                    for ci in range(4):
                        cc = g * 4 + ci
                        cin, Rcin = cin_rot.next()
                        for (tok0, ntok) in blocks:
                            off = tok0 + 2 if tok0 == 0 else tok0 + 6
                            p, Rp = self.psf.next()
                            for k in range(8):
                                mm(p[:, 0:ntok], wq[:, k, ci * 128:(ci + 1) * 128], hT[:, k, tok0:tok0 + ntok], k == 0, k == 7,
                                   [Rwq] + rh(tok0, ntok), [Rp])
                            op("act", "copy", [Rp], [Rcin], out=cin[:, off:off + ntok], in_=p[:, 0:ntok])
                        op("dve", "tensor_scalar", [Rcin, Rcw], [Rcacc], out=cacc[:, 2:2310], in0=cin[:, 0:2308],
                           scalar1=convw[:, cc * 5:cc * 5 + 1], scalar2=None, op0=ALU.mult)
                        for k in range(1, 5):
                            op("dve", "scalar_tensor_tensor", [Rcin, Rcw, Rcacc], [Rcacc], out=cacc[:, 2:2310],
                               in0=cin[:, k:k + 2308], scalar=convw[:, cc * 5 + k:cc * 5 + k + 1], in1=cacc[:, 2:2310],
                               op0=ALU.mult, op1=ALU.add)
                        op("act", "activation", [Rcacc], [RsT], out=siluT[:, ci, 0:256], in_=cacc[:, 2:258], func=AF.Silu)
                        op("act", "activation", [Rcacc], [RsT], out=siluT[:, ci, 256:NTOK], in_=cacc[:, 262:2310], func=AF.Silu)
                    for t in range(NT):
                        pb, Rpb = self.psb.next()
                        for ci in range(4):
                            self.tr(pb[:, ci * 128:(ci + 1) * 128], siluT[:, ci, t * 128:(t + 1) * 128], self.identb[:],
                                    [RsT, self.Ridb], [Rpb])
                        tk, Rtk = tok_rot.next()
                        if t % 2 == 0:
                            op("dve", "tensor_copy", [Rpb], [Rtk], out=tk[:], in_=pb[:, 0:512])
                        else:
                            op("act", "copy", [Rpb], [Rtk], out=tk[:], in_=pb[:, 0:512])
                        self.dma("sp", self.qkv_d[g, t], tk[:], [Rtk], [self.Rqkv[g][t]])
            self.S.barrier()
            with contextlib.ExitStack() as st3:
                wr, Rwr = self.sb(f"wr{l}", [128, 8, 1120], BF16, st3)
                for k0 in (0, 4):
                    self.dma("pool", wr[:, k0:k0 + 4, :], winv[:, k0:k0 + 4, 1536:WIN_COLS], [], [Rwr])
                zt_rot = Rot([self.sb(f"zt{l}_{i}", [128, 512], BF16, st3) for i in range(3)])
                for t in range(NT):
                    p, Rp = self.psf.next()
                    for k in range(8):
                        mm(p[:], hT[:, k, t * 128:(t + 1) * 128], wr[:, k, 0:512], k == 0, k == 7, [Rwr, RhT[t]], [Rp])
                    zt, Rzt = zt_rot.next()
                    op("act", "activation", [Rp], [Rzt], out=zt[:], in_=p[:], func=AF.Silu)
                    self.dma("sp", self.zs_d[t], zt[:], [Rzt], [self.Rzs[t]])
                gates, Rgates = M["gates"]
                nalog, Rnal = self.sb(f"nalog{l}", [128, 16], F32, st3)
                dtb, Rdtb = self.sb(f"dtb{l}", [128, 16], F32, st3)
                self.dma("sp", nalog[:], self.alog_d[l], [], [Rnal])
                self.dma("sp", dtb[:], self.dtb_d[l], [], [Rdtb])
                op("act", "activation", [Rnal], [Rnal], out=nalog[:], in_=nalog[:], func=AF.Exp)
                op("dve", "tensor_scalar", [Rnal], [Rnal], out=nalog[:], in0=nalog[:], scalar1=-1.0, scalar2=None, op0=ALU.mult)
                g1t, Rg1t = self.sb(f"g1t{l}", [128, 9, 16], F32, st3)
                g2t, Rg2t = self.sb(f"g2t{l}", [128, 9, 16], F32, st3)
                for half in range(2):
                    pab, Rpab = self.psf.next()
                    for j in range(9):
                        t = half * 9 + j
                        for k in range(8):
                            mm(pab[:, j * 32:(j + 1) * 32], hT[:, k, t * 128:(t + 1) * 128], wr[:, k, 512:544], k == 0, k == 7,
                               [Rwr, RhT[t]], [Rpab])
                    pv = pab[:, 0:288].rearrange("p (j c) -> p j c", c=32)
                    t0 = half * 9
                    op("dve", "tensor_tensor", [Rpab, Rdtb], [Rg1t], out=g1t[:], in0=pv[:, :, 0:16],
                       in1=dtb[:].unsqueeze(1).to_broadcast([128, 9, 16]), op=ALU.add)
                    op("act", "activation", [Rg1t], [Rg1t], out=g1t[:], in_=g1t[:], func=AF.Exp)
                    op("act", "activation", [Rg1t, self.Rone], [Rg1t], out=g1t[:], in_=g1t[:], func=AF.Ln, bias=self.onec[:])
                    op("dve", "tensor_tensor", [Rg1t, Rnal], [Rgates], out=gates[:, t0:t0 + 9, 0:16], in0=g1t[:],
                       in1=nalog[:].unsqueeze(1).to_broadcast([128, 9, 16]), op=ALU.mult)
                    op("act", "activation", [Rpab], [Rg2t], out=g2t[:], in_=pv[:, :, 16:32], func=AF.Exp, scale=-1.0)
                    op("act", "activation", [Rg2t, self.Rone], [Rg2t], out=g2t[:], in_=g2t[:], func=AF.Ln, bias=self.onec[:])
                    op("act", "activation", [Rg2t], [Rgates], out=gates[:, t0:t0 + 9, 16:32], in_=g2t[:], func=AF.Exp, scale=-0.5)
                self.dump(f"gates{l}", gates[:], [128, NT, 32], [Rgates])
                cqnT, RcqnT = M["cqnT"]
                ckvnT, RckvnT = M["ckvnT"]
                kpeT, RkpeT = M["kpeT"]
                raw_rot = Rot([self.sb(f"cqraw{l}_{i}", [128, 2, 512], F32, st3) for i in range(2)])
                sq_rot = Rot([self.sb(f"cqsq{l}_{i}", [128, 2, 512], BF16, st3) for i in range(2)])
                rst_rot = Rot([self.sb(f"cqrst{l}_{i}", [128, 512], F32, st3) for i in range(2)])
                cs_rot = Rot([self.sb(f"cs{l}_{i}", [128, 2, 512], F32, st3) for i in range(2)])
                rt_rot = Rot([self.sb(f"rt{l}_{i}", [128, 2, 512], F32, st3) for i in range(2)])
                for (tok0, ntok) in blocks:
                    for (nch, wc0, dst, Rdst, inv) in ((2, 544, cqnT, RcqnT, 1.0 / 256), (1, 800, ckvnT, RckvnT, 1.0 / 128)):
                        raw, Rraw = raw_rot.next()
                        sq, Rsq = sq_rot.next()
                        for c in range(nch):
                            p, Rp = self.psf.next()
                            for k in range(8):
                                mm(p[:, 0:ntok], wr[:, k, wc0 + c * 128:wc0 + (c + 1) * 128], hT[:, k, tok0:tok0 + ntok], k == 0, k == 7,
                                   [Rwr] + rh(tok0, ntok), [Rp])
                            op("act", "copy", [Rp], [Rraw], out=raw[:, c, 0:ntok], in_=p[:, 0:ntok])
                        op("dve", "tensor_tensor", [Rraw], [Rsq], out=sq[:, 0:nch, 0:ntok], in0=raw[:, 0:nch, 0:ntok],
                           in1=raw[:, 0:nch, 0:ntok], op=ALU.mult)
                        pss, Rpss = self.psf.next()
                        for c in range(nch):
                            mm(pss[:, 0:ntok], self.onesb[:], sq[:, c, 0:ntok], c == 0, c == nch - 1, [Rsq, self.Ronesb], [Rpss])
                        rst, Rrst = rst_rot.next()
                        op("act", "activation", [Rpss, self.Reps], [Rrst], out=rst[:, 0:ntok], in_=pss[:, 0:ntok], func=AF.Ln,
                           bias=self.epsc[:], scale=inv)
                        op("act", "activation", [Rrst], [Rrst], out=rst[:, 0:ntok], in_=rst[:, 0:ntok], func=AF.Exp, scale=-0.5)
                        for c in range(nch):
                            dv = dst[:, c, tok0:tok0 + ntok] if nch == 2 else dst[:, tok0:tok0 + ntok]
                            op("dve", "tensor_tensor", [Rraw, Rrst], [Rdst], out=dv, in0=raw[:, c, 0:ntok], in1=rst[:, 0:ntok],
                               op=ALU.mult)
                    pk, Rpk = self.psf.next()
                    for k in range(8):
                        mm(pk[0:96, 0:ntok], wr[:, k, 928:1024], hT[:, k, tok0:tok0 + ntok], k == 0, k == 7, [Rwr] + rh(tok0, ntok), [Rpk])
                    if tok0 == 0:
                        op("dve", "tensor_copy", [Rpk], [RkpeT], out=kpeT[64:96, 0:ntok], in_=pk[64:96, 0:ntok])
                    else:
                        s0 = tok0 - 256
                        cs, Rcs = cs_rot.next()
                        self.dma("sp", cs[64:96, 0, :], self.cosT_d[64:96, s0:s0 + 512], [], [Rcs])
                        self.dma("sp", cs[64:96, 1, :], self.sinT_d[64:96, s0:s0 + 512], [], [Rcs])
                        rt, Rrt = rt_rot.next()
                        op("dve", "tensor_tensor", [Rpk, Rcs], [Rrt], out=rt[64:96, 0, :], in0=pk[64:96, 0:512], in1=cs[64:96, 0, :], op=ALU.mult)
                        pkp, Rpkp = self.psf.next()
                        for k in range(8):
                            mm(pkp[0:96, 0:ntok], wr[:, k, 1024:1120], hT[:, k, tok0:tok0 + ntok], k == 0, k == 7, [Rwr] + rh(tok0, ntok),
                               [Rpkp])
                        op("dve", "tensor_tensor", [Rpkp, Rcs], [Rrt], out=rt[64:96, 1, :], in0=pkp[64:96, 0:512], in1=cs[64:96, 1, :], op=ALU.mult)
                        op("dve", "tensor_tensor", [Rrt], [RkpeT], out=kpeT[64:96, tok0:tok0 + 512], in0=rt[64:96, 0, :], in1=rt[64:96, 1, :],
                           op=ALU.add)
                self.dump(f"cqnT{l}", cqnT[:], [128, 2, NTOK], [RcqnT], BF16)
                self.dump(f"ckvnT{l}", ckvnT[:], [128, NTOK], [RckvnT], BF16)
                self.dump(f"kpeT{l}", kpeT[64:96, :], [32, NTOK], [RkpeT], BF16)
            for g in range(3):
                self.dump(f"qkv{l}_{g}", self.qkv_d[g], [NT, 128, 512], self.Rqkv[g], BF16)
            self.dump(f"zs{l}", self.zs_d, [NT, 128, 512], self.Rzs, BF16)
        self.S.barrier()

    def phase_gdn(self, l, last, M):
        op, mm = self.op, self.mm
        gates, Rgates = M["gates"]
        mk = self.mk
        with contextlib.ExitStack() as st:
            sbf = lambda nm, shp, dt=F32: self.sb(f"g{l}_{nm}", shp, dt, st)
            rotp = Rot(self.psf.items[0:3])
            po_t, Rpo = self.psf.items[5]
            pvs_rot = [Rot([(self.psf.items[3 + c][0][:, i * 128:(i + 1) * 128], Res(f"pvs{c}_{i}")) for i in range(4)]) for c in range(2)]
            glT, RglT = sbf("glT", [128, 288])
            gx_rot = Rot([sbf(f"gx{i}", [128, 128]) for i in range(3)])
            pgl, Rpgl = rotp.next()
            for t in range(NT):
                for d in range(2):
                    for hp in range(4):
                        gx, Rgx = gx_rot.next()
                        op("dve", "tensor_copy", [Rgates], [Rgx], out=gx[:].rearrange("p (h k) -> p h k", h=2),
                           in_=gates[:, t, d * 8 + 2 * hp:d * 8 + 2 * hp + 2].unsqueeze(2).to_broadcast([128, 2, 64]))
                        col = ((t * 2 + d) * 4 + hp) * 2
                        mm(pgl[:, col:col + 2], gx[:], self.CI[:], True, True, [Rgx, self.RCI], [Rpgl])
            op("act", "activation", [Rpgl], [RglT], out=glT[:], in_=pgl[:, 0:288], func=AF.Exp)
            self.dump(f"glT{l}", glT[:], [128, 288], [RglT])
            import os as _os
            GD = int(_os.environ.get("GDN_DBG", "99"))
            NTL = int(_os.environ.get("GDN_TILES", "99"))
            raw_rot = Rot([[sbf(f"raw{i}_{j}", [128, 512], BF16) for j in range(3)] for i in range(2)])
            sqt, Rsqt = sbf("sqt", [128, 512])
            st16, Rst16 = sbf("st16", [128, 3, 16])
            eg, Reg = sbf("eg", [128, 16])
            sc, Rsc = sbf("sc", [128, 5, 8])
            kp, Rkp = sbf("kp", [128, 512])
            qn, Rqn = sbf("qn", [128, 512])
            qg, Rqg = sbf("qg", [128, 512])
            vb, Rvb = sbf("vb", [128, 512])
            kbg, Rkbg = sbf("kbg", [128, 768])
            kg, Rkg = sbf("kg", [128, 768])
            kpT, RkpT = sbf("kpT", [128, 4, 128])
            qT, RqT = sbf("qT", [128, 4, 128])
            qgT, RqgT = sbf("qgT", [128, 768])
            vtp, Rvtp = sbf("vtp", [128, 768])
            for (b, R) in ((kbg, Rkbg), (kg, Rkg), (qgT, RqgT), (vtp, Rvtp)):
                op("pool", "memset", [], [R], b[:], 0.0)
            X = [[sbf(f"X{hg}_{i}", [128, 4, 128]) for i in range(5)] for hg in range(2)]
            Bb = [[sbf(f"B{hg}_{i}", [128, 4, 128]) for i in range(2)] for hg in range(2)]
            BTb = [[sbf(f"BT{hg}_{i}", [128, 4, 128]) for i in range(2)] for hg in range(2)]
            Pb = [[sbf(f"P{hg}_{i}", [128, 4, 128]) for i in range(2)] for hg in range(2)]
            ut, Rut = sbf("ut", [128, 512])
            wT, RwT = sbf("wT", [128, 4, 128])
            Sst = [sbf(f"S{hp}", [128, 128]) for hp in range(4)]
            o_rot = Rot([sbf(f"ot{i}", [128, 512]) for i in range(2)])
            ofl_rot = Rot([sbf(f"ofl{i}", [128, 512]) for i in range(2)])
            zs_rot = Rot([sbf(f"zsl{i}", [128, 512], BF16) for i in range(2)])
            yg_rot = Rot([sbf(f"ygo{i}", [128, 512], BF16) for i in range(2)])
            gdng, Rgdng = sbf("gdng", [128, 512])
            self.dma("sp", gdng[:], self.gdng_d[l], [], [Rgdng])
            ost, Rost = sbf("ost", [128, 3, 8])

            def pad4(buf):
                return buf[:].rearrange("p (a c k) -> p a c k", a=4, c=3)[:, :, 0:3:2, :]

            def v4(ap512):
                return ap512.rearrange("p (a c k) -> p a c k", a=4, c=2)

            def bc8(ap8):
                return ap8.unsqueeze(2).to_broadcast([128, 8, 64])

            def bc4(ap8):
                return ap8.rearrange("p (a c) -> p a c", a=4).unsqueeze(3).to_broadcast([128, 4, 2, 64])

            h8 = lambda ap512: ap512.rearrange("p (h k) -> p h k", h=8)
            mb = lambda nm: mk[nm][0][:].unsqueeze(1).to_broadcast([128, 4, 128])

            for d in range(2):
                if GD < 99 and d == 1:
                    break
                order = list(range(NT)) if d == 0 else [1, 0] + list(range(NT - 1, 1, -1))
                order = order[:NTL]
                for hp in range(4):
                    op("dve", "memset", [], [Sst[hp][1]], Sst[hp][0][:], 0.0)
                M_cs, M_kg = ("IU", "SL") if d == 0 else ("IL", "SU")
                M_d1l, M_d1r = ("LE", "GT") if d == 0 else ("GE", "LT")
                M_A, M_AT, M_QK = ("SL", "SU", "IU") if d == 0 else ("SU", "SL", "IL")
                for t in order:
                    g_d = gates[:, t, d * 8:(d + 1) * 8]
                    sb_d = gates[:, t, 16 + d * 8:24 + d * 8]
                    (qr, Rqr), (kr, Rkr), (vr, Rvr) = raw_rot.next()
                    self.dma("sp", qr[:], self.qkv_d[0, t], [self.Rqkv[0][t]], [Rqr])
                    self.dma("sp", kr[:], self.qkv_d[1, t], [self.Rqkv[1][t]], [Rkr])
                    self.dma("sp", vr[:], self.qkv_d[2, t], [self.Rqkv[2][t]], [Rvr])
                    for j, (src, Rs_) in enumerate(((qr, Rqr), (kr, Rkr))):
                        op("dve", "tensor_tensor", [Rs_], [Rsqt], out=sqt[:], in0=src[:], in1=src[:], op=ALU.mult)
                        op("dve", "tensor_reduce", [Rsqt], [Rst16], out=st16[:, 0, j * 8:(j + 1) * 8], in_=h8(sqt[:]), axis=AX.X, op=ALU.add)
                    op("act", "activation", [Rst16, self.Reps], [Rst16], out=st16[:, 1, :], in_=st16[:, 0, :], func=AF.Ln, bias=self.epsc[:])
                    op("act", "activation", [Rst16], [Rst16], out=st16[:, 2, :], in_=st16[:, 1, :], func=AF.Exp, scale=-0.5)
                    rq, rk = st16[:, 2, 0:8], st16[:, 2, 8:16]
                    pgc, Rpgc = rotp.next()
                    mm(pgc[:, 0:8], mk[M_cs][0][:], g_d, True, True, [mk[M_cs][1], Rgates], [Rpgc])
                    mm(pgc[:, 8:16], mk[M_kg][0][:], g_d, True, True, [mk[M_kg][1], Rgates], [Rpgc])
                    op("act", "activation", [Rpgc], [Reg], out=eg[:], in_=pgc[:, 0:16], func=AF.Exp)
                    egc, ekg = eg[:, 0:8], eg[:, 8:16]
                    op("dve", "tensor_tensor", [Rst16, Rgates], [Rsc], out=sc[:, 0, :], in0=rk, in1=sb_d, op=ALU.mult)
                    op("dve", "tensor_scalar", [Rst16], [Rsc], out=sc[:, 1, :], in0=rq, scalar1=0.125, scalar2=None, op0=ALU.mult)
                    op("dve", "tensor_tensor", [Rsc, Reg], [Rsc], out=sc[:, 2, :], in0=sc[:, 1, :], in1=egc, op=ALU.mult)
                    op("dve", "tensor_tensor", [Rsc, Reg], [Rsc], out=sc[:, 3, :], in0=sc[:, 0, :], in1=egc, op=ALU.mult)
                    op("dve", "tensor_tensor", [Rsc, Reg], [Rsc], out=sc[:, 4, :], in0=sc[:, 0, :], in1=ekg, op=ALU.mult)
                    op("dve", "tensor_tensor", [Rkr, Rsc], [Rkp], out=h8(kp[:]), in0=h8(kr[:]), in1=bc8(sc[:, 0, :]), op=ALU.mult)
                    op("dve", "tensor_tensor", [Rqr, Rsc], [Rqn], out=h8(qn[:]), in0=h8(qr[:]), in1=bc8(sc[:, 1, :]), op=ALU.mult)
                    op("dve", "tensor_tensor", [Rqr, Rsc], [Rqg], out=h8(qg[:]), in0=h8(qr[:]), in1=bc8(sc[:, 2, :]), op=ALU.mult)
                    op("dve", "tensor_tensor", [Rvr, Rgates], [Rvb], out=h8(vb[:]), in0=h8(vr[:]), in1=bc8(sb_d), op=ALU.mult)
                    op("dve", "tensor_tensor", [Rkr, Rsc], [Rkbg], out=pad4(kbg), in0=v4(kr[:]), in1=bc4(sc[:, 3, :]), op=ALU.mult)
                    op("dve", "tensor_tensor", [Rkr, Rsc], [Rkg], out=pad4(kg), in0=v4(kr[:]), in1=bc4(sc[:, 4, :]), op=ALU.mult)
                    if GD <= 1:
                        continue
                    for (src, Rs_, dst, Rd_, padded) in ((kp, Rkp, kpT, RkpT, False), (qn, Rqn, qT, RqT, False), (qg, Rqg, qgT, RqgT, True)):
                        pt, Rpt = rotp.next()
                        for hp in range(4):
                            self.tr(pt[:, hp * 128:(hp + 1) * 128], src[:, hp * 128:(hp + 1) * 128], self.identf[:], [Rs_, self.Ridf], [Rpt])
                        if padded:
                            op("act", "copy", [Rpt], [Rd_], out=pad4(dst), in_=v4(pt[:]))
                        else:
                            op("act", "copy", [Rpt], [Rd_], out=dst[:].rearrange("p a k -> p (a k)"), in_=pt[:])
                    if GD <= 2:
                        continue
                    for hg in range(2):
                        X1, X2, X3, X4, X5 = X[hg]
                        gh = gates[:, t, d * 8 + hg:d * 8 + 8:2].unsqueeze(2).to_broadcast([128, 4, 128])
                        op("dve", "tensor_tensor", [Rgates, mk[M_d1r][1]], [X1[1]], out=X1[0][:], in0=gh, in1=mb(M_d1r), op=ALU.mult)
                        op("dve", "tensor_tensor", [Rgates, mk[M_d1l][1]], [X2[1]], out=X2[0][:], in0=gh, in1=mb(M_d1l), op=ALU.mult)
                        pD, RpD = rotp.next()
                        mm(pD[:], mk[M_d1l][0][:], X1[0][:].rearrange("p a k -> p (a k)"), True, True, [X1[1], mk[M_d1l][1]], [RpD])
                        op("act", "activation", [RpD], [X3[1]], out=X3[0][:].rearrange("p a k -> p (a k)"), in_=pD[:], func=AF.Exp)
                        pDT, RpDT = rotp.next()
                        mm(pDT[:], mk[M_d1r][0][:], X2[0][:].rearrange("p a k -> p (a k)"), True, True, [X2[1], mk[M_d1r][1]], [RpDT])
                        op("act", "activation", [RpDT], [X4[1]], out=X4[0][:].rearrange("p a k -> p (a k)"), in_=pDT[:], func=AF.Exp)
                        op("dve", "tensor_tensor", [X3[1], mk[M_A][1]], [X1[1]], out=X1[0][:], in0=X3[0][:], in1=mb(M_A), op=ALU.mult)
                        op("dve", "tensor_tensor", [X4[1], mk[M_AT][1]], [X2[1]], out=X2[0][:], in0=X4[0][:], in1=mb(M_AT), op=ALU.mult)
                        op("dve", "tensor_tensor", [X4[1], mk[M_QK][1]], [X5[1]], out=X5[0][:], in0=X4[0][:], in1=mb(M_QK), op=ALU.mult)
                        pKK, RpKK = rotp.next()
                        pQK, RpQK = rotp.next()
                        for hl in range(4):
                            hp, hh = hl, hg
                            r = slice(hh * 64, (hh + 1) * 64)
                            mm(pKK[:, hl * 128:(hl + 1) * 128], kpT[r, hp, :], kpT[r, hp, :], True, True, [RkpT], [RpKK])
                            mm(pQK[:, hl * 128:(hl + 1) * 128], kpT[r, hp, :], qT[r, hp, :], True, True, [RkpT, RqT], [RpQK])
                        f2 = lambda b: b[0][:].rearrange("p a k -> p (a k)")
                        op("dve", "tensor_tensor", [RpKK, X1[1]], [X3[1]], out=f2(X3), in0=pKK[:], in1=f2(X1), op=ALU.mult)
                        op("dve", "tensor_tensor", [RpKK, X2[1]], [X4[1]], out=f2(X4), in0=pKK[:], in1=f2(X2), op=ALU.mult)
                        op("dve", "tensor_tensor", [RpQK, X5[1]], [X5[1]], out=f2(X5), in0=pQK[:], in1=f2(X5), op=ALU.mult)
                        op("dve", "tensor_tensor", [self.Ridf, X4[1]], [Pb[hg][0][1]], out=Pb[hg][0][0][:],
                           in0=self.identf[:].unsqueeze(1).to_broadcast([128, 4, 128]), in1=X4[0][:], op=ALU.subtract)
                    if GD <= 3:
                        continue
                    cur = {hg: (X[hg][2], X[hg][3], Pb[hg][0]) for hg in range(2)}
                    for lev in range(1, 6):
                        nxt = {}
                        for hg in range(2):
                            Bp, BTp, Pp = cur[hg]
                            Bn = Bb[hg][lev % 2]
                            BTn = BTb[hg][lev % 2]
                            Pn = Pb[hg][lev % 2]
                            f2 = lambda b: b[0][:].rearrange("p a k -> p (a k)")
                            pB, RpB = rotp.next()
                            for hl in range(4):
                                mm(pB[:, hl * 128:(hl + 1) * 128], BTp[0][:, hl, :], Bp[0][:, hl, :], True, True, [BTp[1], Bp[1]], [RpB])
                            op("act", "copy", [RpB], [Bn[1]], out=f2(Bn), in_=pB[:])
                            if lev < 5:
                                pBT, RpBT = rotp.next()
                                for hl in range(4):
                                    mm(pBT[:, hl * 128:(hl + 1) * 128], Bp[0][:, hl, :], BTp[0][:, hl, :], True, True, [BTp[1], Bp[1]], [RpBT])
                                op("act", "copy", [RpBT], [BTn[1]], out=f2(BTn), in_=pBT[:])
                            pP, RpP = rotp.next()
                            for hl in range(4):
                                mm(pP[:, hl * 128:(hl + 1) * 128], Bn[0][:, hl, :], Pp[0][:, hl, :], True, True, [Bn[1], Pp[1]], [RpP])
                            op("dve", "tensor_tensor", [RpP, Pp[1]], [Pn[1]], out=f2(Pn), in0=pP[:], in1=f2(Pp), op=ALU.add)
                            nxt[hg] = (Bn, BTn, Pn)
                        cur = nxt
                    if GD <= 4:
                        continue
                    pu, Rpu = rotp.next()
                    pw, Rpw = rotp.next()
                    for h in range(8):
                        TT = cur[h % 2][2]
                        mm(pu[:, h * 64:(h + 1) * 64], TT[0][:, h // 2, :], vb[:, h * 64:(h + 1) * 64], True, True, [TT[1], Rvb], [Rpu])
                    for hp in range(4):
                        TT0, TT1 = cur[0][2], cur[1][2]
                        mm(pw[:, hp * 128:(hp + 1) * 128], kbg[:, hp * 192:hp * 192 + 128], TT0[0][:, hp, :], True, False,
                           [TT0[1], Rkbg], [Rpw])
                        mm(pw[:, hp * 128:(hp + 1) * 128], kbg[:, hp * 192 + 64:hp * 192 + 192], TT1[0][:, hp, :], False, True,
                           [TT1[1], Rkbg], [Rpw])
                    op("act", "copy", [Rpu], [Rut], out=ut[:], in_=pu[:])
                    op("act", "copy", [Rpw], [RwT], out=wT[:].rearrange("p a k -> p (a k)"), in_=pw[:])
                    if GD <= 5:
                        continue
                    cs = [0, 1] if d == 0 else [1, 0]
                    for hp in range(4):
                        S_, RS = Sst[hp]
                        for ci, c in enumerate(cs):
                            r = slice(c * 64, (c + 1) * 64)
                            pv, Rpv = pvs_rot[c].next()
                            mm(pv, wT[:, hp, :], S_[:], True, True, [RwT, RS], [Rpv])
                            vview = vtp[r, hp * 192:(hp + 1) * 192].rearrange("p (c k) -> p c k", c=3)[:, 0:3:2, :]
                            op("dve", "tensor_tensor", [Rut, Rpv], [Rvtp], out=vview,
                               in0=ut[r, hp * 128:(hp + 1) * 128].rearrange("p (c k) -> p c k", c=2),
                               in1=pv[r, :].rearrange("p (c k) -> p c k", c=2), op=ALU.subtract)
                            mm(po_t[:, hp * 128:(hp + 1) * 128], qgT[:, hp * 192 + c * 64:hp * 192 + c * 64 + 128], S_[:], ci == 0, False,
                               [RqgT, RS], [Rpo])
                            ps_, Rps = pvs_rot[c].next()
                            mm(ps_, kg[r, hp * 192:hp * 192 + 128], vtp[r, hp * 192:hp * 192 + 128], True, False, [Rkg, Rvtp], [Rps])
                            mm(ps_, kg[r, hp * 192 + 64:hp * 192 + 192], vtp[r, hp * 192 + 64:hp * 192 + 192], False, True, [Rkg, Rvtp], [Rps])
                            col = ((t * 2 + d) * 4 + hp) * 2 + c
                            op("dve", "scalar_tensor_tensor", [RS, RglT, Rps], [RS], out=S_[:], in0=S_[:], scalar=glT[:, col:col + 1],
                               in1=ps_, op0=ALU.mult, op1=ALU.add)
                        for hh in range(2):
                            h = 2 * hp + hh
                            QK = X[hh][4]
                            mm(po_t[:, h * 64:(h + 1) * 64], QK[0][:, hp, :], vtp[:, hp * 192 + hh * 128:hp * 192 + hh * 128 + 64], False, True,
                               [QK[1], Rvtp], [Rpo])
                    if GD <= 6:
                        continue
                    if d == 0:
                        ot, Rot_ = o_rot.next()
                        op("act", "copy", [Rpo], [Rot_], out=ot[:], in_=po_t[:])
                        self.dma("sp", self.of_d[t], ot[:], [Rot_], [self.Rof[t]])
                    else:
                        ofl, Rofl = ofl_rot.next()
                        self.dma("sp", ofl[:], self.of_d[t], [self.Rof[t]], [Rofl])
                        if last and t < 2:
                            continue
                        zsl, Rzsl = zs_rot.next()
                        self.dma("sp", zsl[:], self.zs_d[t], [self.Rzs[t]], [Rzsl])
                        ot, Rot_ = o_rot.next()
                        op("dve", "tensor_tensor", [Rpo, Rofl], [Rot_], out=ot[:], in0=po_t[:], in1=ofl[:], op=ALU.add)
                        op("dve", "tensor_tensor", [Rot_], [Rsqt], out=sqt[:], in0=ot[:], in1=ot[:], op=ALU.mult)
                        op("dve", "tensor_reduce", [Rsqt], [Rost], out=ost[:, 0, :], in_=h8(sqt[:]), axis=AX.X, op=ALU.add)
                        op("act", "activation", [Rost, self.Reps], [Rost], out=ost[:, 1, :], in_=ost[:, 0, :], func=AF.Ln, bias=self.epsc[:],
                           scale=1.0 / 64)
                        op("act", "activation", [Rost], [Rost], out=ost[:, 2, :], in_=ost[:, 1, :], func=AF.Exp, scale=-0.5)
                        op("dve", "tensor_tensor", [Rot_, Rost], [Rot_], out=h8(ot[:]), in0=h8(ot[:]), in1=bc8(ost[:, 2, :]), op=ALU.mult)
                        op("dve", "tensor_tensor", [Rzsl, Rgdng], [Rsqt], out=sqt[:], in0=zsl[:], in1=gdng[:], op=ALU.mult)
                        yg, Ryg_ = yg_rot.next()
                        op("dve", "tensor_tensor", [Rot_, Rsqt], [Ryg_], out=yg[:], in0=ot[:], in1=sqt[:], op=ALU.mult)
                        self.dma("sp", self.yg_d[t], yg[:], [Ryg_], [self.Ryg[t]])
            self.dump(f"of{l}", self.of_d, [NT, 128, 512], self.Rof)
            self.dump(f"ygd{l}", self.yg_d, [NT, 128, 512], self.Ryg, BF16)
        self.S.barrier()

    def phase_mla(self, l, last, M):
        op, mm = self.op, self.mm
        cqnT, RcqnT = M["cqnT"]
        ckvnT, RckvnT = M["ckvnT"]
        kpeT, RkpeT = M["kpeT"]
        omla, Romla = M["omla"]
        blocks = [(0, 256), (256, 512), (768, 512), (1280, 512), (1792, 512)]
        with contextlib.ExitStack() as st:
            sbf = lambda nm, shp, dt=F32: self.sb(f"a{l}_{nm}", shp, dt, st)
            rot4 = Rot(self.psf.items[0:4])
            po_rot = Rot(self.psf.items[4:6])
            wuq, Rwuq = sbf("wuq", [128, 2, 1536], BF16)
            wk, Rwk = sbf("wk", [128, 512], BF16)
            wv, Rwv = sbf("wv", [128, 512], BF16)
            qng, Rqng = sbf("qng", [128, 2])
            kvng, Rkvng = sbf("kvng", [128, 1])
            self.dma("sp", qng[:], self.qngT_d[l], [], [Rqng])
            self.dma("sp", kvng[:], self.kvngT_d[l], [], [Rkvng])
            self.dma("pool", wuq[:], self.wuq_d[l].rearrange("(c p) n -> p c n", p=128), [], [Rwuq])
            self.dma("pool", wk[:], self.wukvk_d[l], [], [Rwk])
            self.dma("pool", wv[:], self.wukvv_d[l], [], [Rwv])
            for c in range(2):
                op("dve", "tensor_scalar", [Rwuq, Rqng], [Rwuq], out=wuq[:, c, :], in0=wuq[:, c, :], scalar1=qng[:, c:c + 1], scalar2=None,
                   op0=ALU.mult)
            op("dve", "tensor_scalar", [Rwk, Rkvng], [Rwk], out=wk[:], in0=wk[:], scalar1=kvng[:, 0:1], scalar2=None, op0=ALU.mult)
            op("dve", "tensor_scalar", [Rwv, Rkvng], [Rwv], out=wv[:], in0=wv[:], scalar1=kvng[:, 0:1], scalar2=None, op0=ALU.mult)
            cosb, Rcos = sbf("cosb", [128, T])
            sinb, Rsin = sbf("sinb", [128, T])
            self.dma("sp", cosb[64:96, :], self.cosT_d[64:96, :], [], [Rcos])
            self.dma("sp", sinb[64:96, :], self.sinT_d[64:96, :], [], [Rsin])
            Vaug, RV = sbf("Vaug", [128, NT, 8, 65], BF16)
            op("pool", "memset", [], [RV], Vaug[:, :, :, 64:65], 1.0)
            for t in range(NT):
                p, Rp = rot4.next()
                mm(p[:], ckvnT[:, t * 128:(t + 1) * 128], wv[:], True, True, [RckvnT, Rwv], [Rp])
                if t % 2 == 0:
                    op("act", "copy", [Rp, RV], [RV], out=Vaug[:, t, :, 0:64], in_=p[:].rearrange("p (h k) -> p h k", h=8))
                else:
                    op("dve", "tensor_copy", [Rp, RV], [RV], out=Vaug[:, t, :, 0:64], in_=p[:].rearrange("p (h k) -> p h k", h=8))
            qT_rot = Rot([sbf(f"qT{i}", [128, NTOK], BF16) for i in range(2)])
            kT_rot = Rot([sbf(f"kT{i}", [128, NTOK], BF16) for i in range(2)])
            pT_rot = Rot([sbf(f"pT{i}", [128, 512], BF16) for i in range(4)])
            rt_rot = Rot([sbf(f"rt{i}", [128, 2, 512]) for i in range(2)])
            rden, Rrden = sbf("rden", [128, 4])
            zerob, Rzb = sbf("zerob", [128, 260], BF16)
            op("pool", "memset", [], [Rzb], zerob[:], 0.0)
            scale = 96.0 ** -0.5
            for h in range(8):
                kT, RkT = kT_rot.next()
                qT, RqT = qT_rot.next()
                for (tok0, ntok) in blocks:
                    p, Rp = rot4.next()
                    mm(p[0:64, 0:ntok], wk[:, h * 64:(h + 1) * 64], ckvnT[:, tok0:tok0 + ntok], True, True, [Rwk, RckvnT], [Rp])
                    op("act", "copy", [Rp], [RkT], out=kT[0:64, tok0:tok0 + ntok], in_=p[0:64, 0:ntok])
                op("dve", "tensor_copy", [RkpeT], [RkT], out=kT[64:96, :], in_=kpeT[64:96, :])
                for bi, (tok0, ntok) in enumerate(blocks):
                    if bi == 0 and last:
                        continue
                    pA, RpA = rot4.next()
                    for c in range(2):
                        mm(pA[0:96, 0:ntok], wuq[:, c, h * 192:h * 192 + 96], cqnT[:, c, tok0:tok0 + ntok], c == 0, c == 1,
                           [Rwuq, RcqnT], [RpA])
                    op("act", "copy", [RpA], [RqT], out=qT[0:64, tok0:tok0 + ntok], in_=pA[0:64, 0:ntok])
                    if bi == 0:
                        op("dve", "tensor_copy", [RpA], [RqT], out=qT[64:96, tok0:tok0 + ntok], in_=pA[64:96, 0:ntok])
                    else:
                        s0 = tok0 - 256
                        rt, Rrt = rt_rot.next()
                        op("dve", "tensor_tensor", [RpA, Rcos], [Rrt], out=rt[64:96, 0, :], in0=pA[64:96, 0:512], in1=cosb[64:96, s0:s0 + 512],
                           op=ALU.mult)
                        pB, RpB = rot4.next()
                        for c in range(2):
                            mm(pB[0:96, 0:ntok], wuq[:, c, h * 192 + 96:h * 192 + 192], cqnT[:, c, tok0:tok0 + ntok], c == 0, c == 1,
                               [Rwuq, RcqnT], [RpB])
                        op("dve", "tensor_tensor", [RpB, Rsin], [Rrt], out=rt[64:96, 1, :], in0=pB[64:96, 0:512], in1=sinb[64:96, s0:s0 + 512],
                           op=ALU.mult)
                        op("dve", "tensor_tensor", [Rrt], [RqT], out=qT[64:96, tok0:tok0 + 512], in0=rt[64:96, 0, :], in1=rt[64:96, 1, :],
                           op=ALU.add)
                for qb, (tok0, ntok) in enumerate(blocks):
                    if qb == 0 and last:
                        continue
                    nsub = ntok // 128
                    jts = [0, 1] if qb == 0 else list(range(NT))
                    po, Rpo = po_rot.next()
                    pov = po[:, 0:nsub * 65].rearrange("p (s k) -> p s k", k=65)
                    mm(po[:, 0:nsub * 65], zerob[:, 0:128], zerob[:, 0:nsub * 65], True, False, [Rzb], [Rpo])
                    for ji, jt in enumerate(jts):
                        ps_, Rps = rot4.next()
                        mm(ps_[:, 0:ntok], kT[0:96, jt * 128:(jt + 1) * 128], qT[0:96, tok0:tok0 + ntok], True, True, [RkT, RqT], [Rps])
                        pT, RpT = pT_rot.next()
                        op("act", "activation", [Rps], [RpT], out=pT[:, 0:ntok], in_=ps_[:, 0:ntok], func=AF.Exp, scale=scale)
                        for isub in range(nsub):
                            mm(pov[:, isub, :], pT[:, isub * 128:(isub + 1) * 128], Vaug[:, jt, h, :], False, ji == len(jts) - 1,
                               [RpT, RV], [Rpo])
                    t0 = tok0 // 128
                    op("dve", "reciprocal", [Rpo], [Rrden], out=rden[:, 0:nsub], in_=pov[:, :, 64])
                    op("dve", "tensor_tensor", [Rpo, Rrden], [Romla], out=omla[:, t0:t0 + nsub, h * 64:(h + 1) * 64], in0=pov[:, :, 0:64],
                       in1=rden[:, 0:nsub].unsqueeze(2).to_broadcast([128, nsub, 64]), op=ALU.mult)
            self.dump(f"omla{l}", omla[:], [128, NT, 512], [Romla], BF16)
        self.S.barrier()

    def phase_out(self, l, last, M):
        op, mm = self.op, self.mm
        omla, Romla = M["omla"]
        self.make_gbc(l, "g1")
        tiles = list(range(2, NT)) if last else list(range(NT))
        with contextlib.ExitStack() as st:
            sbf = lambda nm, shp, dt=F32: self.sb(f"o{l}_{nm}", shp, dt, st)
            wo, Rwo = sbf("wo", [128, 8, D], BF16)
            wov = self.wout_d[l].rearrange("(k p) n -> p k n", p=128)
            for k0 in (0, 4):
                self.dma("pool", wo[:, k0:k0 + 4, :], wov[:, k0:k0 + 4, :], [], [Rwo])
            yg_rot = Rot([sbf(f"yg{i}", [128, 512], BF16) for i in range(3)])
            ym_rot = Rot([sbf(f"ym{i}", [128, 8, 128], BF16) for i in range(3)])
            tmp_rot = Rot([sbf(f"tmp{i}", [128, 512]) for i in range(3)])
            for t in tiles:
                kind = 1 if t < 2 else 0
                yg, Ryg_ = yg_rot.next()
                self.dma("sp", yg[:], self.yg_d[t], [self.Ryg[t]], [Ryg_])
                pb, Rpb = self.psb.next()
                for c in range(4):
                    self.tr(pb[:, c * 128:(c + 1) * 128], yg[:, c * 128:(c + 1) * 128], self.identb[:], [Ryg_, self.Ridb], [Rpb])
                for c in range(4):
                    self.tr(pb[:, 512 + c * 128:512 + (c + 1) * 128], omla[:, t, c * 128:(c + 1) * 128], self.identb[:], [Romla, self.Ridb], [Rpb])
                ym, Rym = ym_rot.next()
                op("act", "copy", [Rpb], [Rym], out=ym[:].rearrange("p a k -> p (a k)"), in_=pb[:])
                g1, Rg1 = self.gbc[kind]
                for nb in range(2):
                    py, Rpy = self.psf.next()
                    for k in range(8):
                        mm(py[:], ym[:, k, :], wo[:, k, nb * 512:(nb + 1) * 512], k == 0, k == 7, [Rym, Rwo], [Rpy])
                    tmp, Rtmp = tmp_rot.next()
                    op("dve", "tensor_tensor", [Rpy, Rg1], [Rtmp], out=tmp[:], in0=py[:], in1=g1[:, nb * 512:(nb + 1) * 512], op=ALU.mult)
                    op("dve", "tensor_tensor", [Rtmp, self.Rx[t]], [self.Rx[t]], out=self.xs[:, t, nb * 512:(nb + 1) * 512],
                       in0=self.xs[:, t, nb * 512:(nb + 1) * 512], in1=tmp[:], op=ALU.add)
            for t in tiles:
                self.dump(f"xmix{l}_{t}", self.xs[:, t, :], [128, D], [self.Rx[t]])
        self.S.barrier()

    def phase_final(self):
        op = self.op
        with contextlib.ExitStack() as st:
            self.fng, self.Rfng = self.sb("fng_sb", [128, D], F32, st)
            self.dma("sp", self.fng[:], self.fng_d, [], [self.Rfng])
            junk_rot = Rot([self.sb(f"fjunk{i}", [128, D], BF16, st) for i in range(2)])
            o_rot = Rot([self.sb(f"fo{i}", [128, D], F32, st) for i in range(3)])
            ss = self.stat[:, 0, :]
            ln = self.stat[:, 1, :]
            rs = self.stat[:, 2, :]
            op("dve", "memset", [], self.Rss[2:NT], ss[:, 2:NT], 0.0)
            for t in range(2, NT):
                j, Rj = junk_rot.next()
                op("act", "activation", [self.Rx[t]], [Rj, self.Rss[t]], out=j[:], in_=self.xs[:, t, :], func=AF.Square,
                   accum_out=ss[:, t:t + 1])
            op("act", "activation", self.Rss[2:NT] + [self.Reps], [self.Rln], out=ln[:, 2:NT], in_=ss[:, 2:NT], func=AF.Ln,
               bias=self.epsc[:], scale=1.0 / D)
            op("act", "activation", [self.Rln], [self.Rrs], out=rs[:, 2:NT], in_=ln[:, 2:NT], func=AF.Exp, scale=-0.5)
            ov = self.out_d.rearrange("(t p) d -> p t d", p=128)
            for t in range(2, NT):
                o, Ro = o_rot.next()
                op("dve", "scalar_tensor_tensor", [self.Rx[t], self.Rrs, self.Rfng], [Ro], out=o[:], in0=self.xs[:, t, :],
                   scalar=rs[:, t:t + 1], in1=self.fng[:], op0=ALU.mult, op1=ALU.mult)
                self.out_ops.append(self.dma("sp", ov[:, t - 2, :], o[:], [Ro], []))


def _rope_tables():
    rows = T // 64
    row = np.repeat(np.arange(rows), 64).astype(np.float32)
    col = np.tile(np.arange(64), rows).astype(np.float32)
    inv_freq = (10000.0 ** (-np.arange(0, 16, 2, dtype=np.float32) / 16)).astype(np.float32)

    def ax(pos):
        a = pos[:, None] * inv_freq[None, :]
        return np.concatenate([a, a], axis=-1)

    ang = np.concatenate([ax(row), ax(col)], axis=-1).astype(np.float32)
    cos = np.cos(ang).astype(np.float32)
    sin = np.sin(ang).astype(np.float32)
    sign = np.where((np.arange(32) % 16) < 8, -1.0, 1.0).astype(np.float32)
    cosT = np.zeros((128, T), np.float32)
    sinT = np.zeros((128, T), np.float32)
    cosT[64:96] = cos.T
    sinT[64:96] = (sin * sign[None, :]).T
    return cosT, sinT


def _perm32():
    i = np.arange(32)
    return np.where((i % 16) < 8, i + 8, i - 8)


def prep_inputs(x, c, ctx, c_ctx, w_mod, b_mod, norm1_g, norm2_g, w_in, conv_w, a_log, dt_bias, gdn_norm_g,
                q_norm_g, w_uq, kv_norm_g, w_ukv, w_out, w_gate, w_up, w_down, final_norm_g):
    f = lambda a: np.ascontiguousarray(np.asarray(a, dtype=np.float32))
    x, c, ctx, c_ctx = f(x), f(c), f(ctx), f(c_ctx)
    w_in = f(w_in)
    perm = _perm32()
    kr = w_in[:, :, 2464:2496]
    z64 = np.zeros((NL, D, 64), np.float32)
    w_in_ext = np.concatenate([w_in[:, :, 0:2464], z64, kr, z64, kr[:, :, perm]], axis=2)
    assert w_in_ext.shape[2] == WIN_COLS
    w_uq = f(w_uq).reshape(NL, 256, 8, 96)
    zq = np.zeros((NL, 256, 8, 64), np.float32)
    w_uq_ext = np.concatenate([w_uq, zq, w_uq[:, :, :, 64:96][:, :, :, perm]], axis=3).reshape(NL, 256, 8 * 192)
    w_ukv = f(w_ukv).reshape(NL, 128, 8, 128)
    cosT, sinT = _rope_tables()
    shared = {
        "w_mod": f(w_mod),
        "b_modT": f(np.transpose(f(b_mod).reshape(NL, 48, 128), (0, 2, 1))),
        "n1T": f(np.transpose(f(norm1_g).reshape(NL, 8, 128), (0, 2, 1))),
        "n2T": f(np.transpose(f(norm2_g).reshape(NL, 8, 128), (0, 2, 1))),
        "fng": f(np.broadcast_to(f(final_norm_g)[None, :], (128, D))),
        "w_in_ext": f(w_in_ext),
        "convT": f(np.transpose(f(conv_w).reshape(NL, 5, 12, 128), (0, 3, 2, 1)).reshape(NL, 128, 60)),
        "alog_bc": f(np.broadcast_to(f(a_log).reshape(NL, 1, 16), (NL, 128, 16))),
        "dtb_bc": f(np.broadcast_to(f(dt_bias).reshape(NL, 1, 16), (NL, 128, 16))),
        "gdng_bc": f(np.broadcast_to(np.tile(f(gdn_norm_g), (1, 8))[:, None, :], (NL, 128, 512))),
        "qngT": f(np.transpose(f(q_norm_g).reshape(NL, 2, 128), (0, 2, 1))),
        "kvngT": f(f(kv_norm_g).reshape(NL, 128, 1)),
        "w_uq_ext": f(w_uq_ext),
        "w_ukv_k": f(w_ukv[:, :, :, 0:64].reshape(NL, 128, 512)),
        "w_ukv_v": f(w_ukv[:, :, :, 64:128].reshape(NL, 128, 512)),
        "w_out": f(w_out), "w_gate": f(w_gate), "w_up": f(w_up), "w_down": f(w_down),
        "cosT": cosT, "sinT": sinT,
    }
    maps = []
    for b in range(x.shape[0]):
        m = dict(shared)
        m["x"] = f(x[b])
        m["ctx"] = f(ctx[b])
        m["cT"] = f(np.stack([c[b].reshape(8, 128).T, c_ctx.reshape(8, 128).T], axis=2).reshape(128, 16))
        maps.append(m)
    return maps


_NC_CACHE = {}


def kernel(**inputs):
    maps = prep_inputs(**inputs)
    if "nc" not in _NC_CACHE:
        _NC_CACHE["nc"] = Builder().build()
    nc = _NC_CACHE["nc"]
    res = run_bass_kernel_spmd(nc, maps, core_ids=list(range(len(maps))))
    return np.stack([np.asarray(r["out"], dtype=np.float32) for r in res.results], axis=0)
```
